# Optimizing a Trainium2 kernel written in Bass

```python
import jax, jax.numpy as jnp
from jax import lax
import numpy as np

D_MODEL = 2048
BATCH = 8
SEQ = 4096
DEPTH = 2

EXPAND = 2
D_INNER = EXPAND * D_MODEL
N_MIXERS = 2
N_GLA = (DEPTH + 1) // 2
N_RET = DEPTH // 2
EPS = 1e-6
GLA_HEADS = 4
GLA_DK = D_MODEL // 2
GLA_HEAD_K = GLA_DK // GLA_HEADS
GLA_HEAD_V = D_INNER // GLA_HEADS
GLA_GATE_RANK = 16
GLA_GATE_TEMP = 16.0
GLA_CHUNK = 64
RET_HEADS = 8
RET_DK = D_MODEL
RET_HEAD_K = RET_DK // RET_HEADS
RET_HEAD_V = D_INNER // RET_HEADS
RET_CHUNK = 64
ROPE_BASE = 10000.0

kernel_name = "hybrid_gla_retnet_trunk"


def rms_norm(x, g):
    xf = x.astype(jnp.float32)
    xn = xf * lax.rsqrt(jnp.mean(xf * xf, axis=-1, keepdims=True) + EPS)
    return xn.astype(x.dtype) * g


def to_chunks(t, c):
    b, s, h, d = t.shape
    return t.reshape(b, s // c, c, h, d).transpose(1, 0, 3, 2, 4)


def from_chunks(t):
    nc, b, h, c, d = t.shape
    return t.transpose(1, 0, 3, 2, 4).reshape(b, nc * c, h, d)


def gla_mixer(h, w_in, w_a1, w_a2, b_a, head_g, w_out):
    B, S, _ = h.shape
    proj = h @ w_in
    q, k, v, gate = jnp.split(proj, [GLA_DK, 2 * GLA_DK, 2 * GLA_DK + D_INNER], axis=-1)
    log_a = jax.nn.log_sigmoid(((h @ w_a1) @ w_a2 + b_a).astype(jnp.float32)) / GLA_GATE_TEMP
    hd = lambda t, d: t.reshape(B, S, GLA_HEADS, d).astype(jnp.float32)
    qc = to_chunks(hd(q, GLA_HEAD_K) * (GLA_HEAD_K ** -0.5), GLA_CHUNK)
    kc = to_chunks(hd(k, GLA_HEAD_K), GLA_CHUNK)
    vc = to_chunks(hd(v, GLA_HEAD_V), GLA_CHUNK)
    ac = to_chunks(hd(log_a, GLA_HEAD_K), GLA_CHUNK)
    causal = jnp.tril(jnp.ones((GLA_CHUNK, GLA_CHUNK), dtype=bool))[:, :, None]

    def step(state, inp):
        qb, kb, vb, ab = inp
        cum = jnp.cumsum(ab, axis=-2)
        last = cum[..., -1:, :]
        inter = jnp.einsum('bhtk,bhkv->bhtv', qb * jnp.exp(cum), state)
        diff = cum[:, :, :, None, :] - cum[:, :, None, :, :]
        decay = jnp.where(causal, jnp.exp(jnp.where(causal, diff, 0.0)), 0.0)
        scores = jnp.einsum('bhtk,bhsk,bhtsk->bhts', qb, kb, decay)
        intra = jnp.einsum('bhts,bhsv->bhtv', scores, vb)
        new_state = jnp.exp(last)[:, :, 0, :, None] * state + jnp.einsum(
            'bhsk,bhsv->bhkv', kb * jnp.exp(last - cum), vb)
        return new_state, inter + intra

    s0 = jnp.zeros((B, GLA_HEADS, GLA_HEAD_K, GLA_HEAD_V), jnp.float32)
    _, o = lax.scan(step, s0, (qc, kc, vc, ac))
    o = from_chunks(o)
    o = o * lax.rsqrt(jnp.mean(o * o, axis=-1, keepdims=True) + EPS)
    o = o.reshape(B, S, D_INNER).astype(h.dtype) * head_g
    return (o * jax.nn.silu(gate)) @ w_out


def apply_rotary(t, cos, sin):
    t1, t2 = jnp.split(t, 2, axis=-1)
    return jnp.concatenate([t1 * cos - t2 * sin, t2 * cos + t1 * sin], axis=-1)


def retention_mixer(h, positions, w_in, gn_g, gn_b, w_out):
    B, S, _ = h.shape
    proj = h @ w_in
    q, k, v, gate = jnp.split(proj, [RET_DK, 2 * RET_DK, 2 * RET_DK + D_INNER], axis=-1)
    q = q.reshape(B, S, RET_HEADS, RET_HEAD_K).astype(jnp.float32)
    k = k.reshape(B, S, RET_HEADS, RET_HEAD_K).astype(jnp.float32) * (RET_HEAD_K ** -0.5)
    v = v.reshape(B, S, RET_HEADS, RET_HEAD_V).astype(jnp.float32)
    inv_freq = ROPE_BASE ** (-jnp.arange(RET_HEAD_K // 2, dtype=jnp.float32) / (RET_HEAD_K // 2))
    ang = positions.astype(jnp.float32)[..., None] * inv_freq
    cos, sin = jnp.cos(ang)[:, :, None, :], jnp.sin(ang)[:, :, None, :]
    q, k = apply_rotary(q, cos, sin), apply_rotary(k, cos, sin)

    log_gamma = jnp.log1p(-jnp.exp2(-5.0 - jnp.arange(RET_HEADS, dtype=jnp.float32)))
    idx = jnp.arange(RET_CHUNK, dtype=jnp.float32)
    dpos = idx[:, None] - idx[None, :]
    dmat = jnp.where(dpos >= 0, jnp.exp(log_gamma[:, None, None] * jnp.where(dpos >= 0, dpos, 0.0)), 0.0)
    xi = jnp.exp(log_gamma[:, None] * (idx + 1.0))[..., None]
    zeta = jnp.exp(log_gamma[:, None] * (RET_CHUNK - 1.0 - idx))[..., None]
    g_chunk = jnp.exp(log_gamma * RET_CHUNK)[:, None, None]

    def step(state, inp):
        qb, kb, vb = inp
        scores = jnp.einsum('bhtk,bhsk->bhts', qb, kb) * dmat
        intra = jnp.einsum('bhts,bhsv->bhtv', scores, vb)
        inter = jnp.einsum('bhtk,bhkv->bhtv', qb, state) * xi
        new_state = g_chunk * state + jnp.einsum('bhsk,bhsv->bhkv', kb * zeta, vb)
        return new_state, intra + inter

    s0 = jnp.zeros((B, RET_HEADS, RET_HEAD_K, RET_HEAD_V), jnp.float32)
    _, o = lax.scan(step, s0, (to_chunks(q, RET_CHUNK), to_chunks(k, RET_CHUNK), to_chunks(v, RET_CHUNK)))
    o = from_chunks(o)
    mu = jnp.mean(o, axis=-1, keepdims=True)
    var = jnp.mean(jnp.square(o - mu), axis=-1, keepdims=True)
    o = ((o - mu) * lax.rsqrt(var + EPS)).reshape(B, S, D_INNER).astype(h.dtype)
    o = o * gn_g + gn_b
    return (o * jax.nn.silu(gate)) @ w_out


def setup_inputs(seed: int = 0) -> dict:
    key = jax.random.key(seed)
    ks = jax.random.split(key, 16)
    nrm = lambda k, shape, fan_in: jax.random.normal(k, shape, jnp.float32) * fan_in ** -0.5
    x = jax.random.normal(ks[0], (BATCH, SEQ, D_MODEL), jnp.float32)
    offsets = jax.random.randint(ks[1], (BATCH, 1), 0, 1024, dtype=jnp.int32)
    positions = offsets + jnp.arange(SEQ, dtype=jnp.int32)[None, :]
    gla_in = 2 * GLA_DK + 2 * D_INNER
    ret_in = 2 * RET_DK + 2 * D_INNER
    return {
        "x": x,
        "positions": positions,
        "gla_norm": 1.0 + 0.02 * jax.random.normal(ks[2], (N_GLA, D_MODEL), jnp.float32),
        "gla_w_in": nrm(ks[3], (N_GLA, D_MODEL, gla_in), D_MODEL),
        "gla_w_a1": nrm(ks[4], (N_GLA, D_MODEL, GLA_GATE_RANK), D_MODEL),
        "gla_w_a2": nrm(ks[5], (N_GLA, GLA_GATE_RANK, GLA_DK), GLA_GATE_RANK),
        "gla_b_a": 0.1 * jax.random.normal(ks[6], (N_GLA, GLA_DK), jnp.float32),
        "gla_head_g": 1.0 + 0.02 * jax.random.normal(ks[7], (N_GLA, D_INNER), jnp.float32),
        "gla_w_out": nrm(ks[8], (N_GLA, D_INNER, D_MODEL), D_INNER),
        "ret_norm": 1.0 + 0.02 * jax.random.normal(ks[9], (N_RET, D_MODEL), jnp.float32),
        "ret_w_in": nrm(ks[10], (N_RET, D_MODEL, ret_in), D_MODEL),
        "ret_gn_g": 1.0 + 0.02 * jax.random.normal(ks[11], (N_RET, D_INNER), jnp.float32),
        "ret_gn_b": 0.02 * jax.random.normal(ks[12], (N_RET, D_INNER), jnp.float32),
        "ret_w_out": nrm(ks[13], (N_RET, D_INNER, D_MODEL), D_INNER),
        "final_norm": 1.0 + 0.02 * jax.random.normal(ks[14], (D_MODEL,), jnp.float32),
    }


def reference(x, positions, gla_norm, gla_w_in, gla_w_a1, gla_w_a2, gla_b_a, gla_head_g, gla_w_out,
              ret_norm, ret_w_in, ret_gn_g, ret_gn_b, ret_w_out, final_norm):
    h = x
    for i in range(DEPTH):
        j = i // N_MIXERS
        if i % N_MIXERS == 0:
            h = h + gla_mixer(rms_norm(h, gla_norm[j]), gla_w_in[j], gla_w_a1[j], gla_w_a2[j],
                              gla_b_a[j], gla_head_g[j], gla_w_out[j])
        else:
            h = h + retention_mixer(rms_norm(h, ret_norm[j]), positions, ret_w_in[j],
                                    ret_gn_g[j], ret_gn_b[j], ret_w_out[j])
    return rms_norm(h, final_norm)
```

```python
import math
from contextlib import ExitStack

import numpy as np
import ml_dtypes
import concourse.bass as bass
import concourse.mybir as mybir
from concourse.bass_utils import run_bass_kernel_spmd

F32 = mybir.dt.float32
BF16 = mybir.dt.bfloat16
I32 = mybir.dt.int32
U8 = mybir.dt.uint8
AF = mybir.ActivationFunctionType
ALU = mybir.AluOpType

D = 2048
DI = 4096
KC = D // 128
EPS = 1e-6
SEQ = 4096
_DBG_STAGE = 99
_OPLIMIT = 10 ** 9
GLA_H, GLA_DK, GLA_DV = 4, 256, 1024
RET_H, RET_DK, RET_DV = 8, 256, 512


class Grp:
    def __init__(self, sem):
        self.sem = sem
        self.cum = 0


class Tile:
    def __init__(self, name, ap, gkey=None):
        self.name = name
        self.ap = ap
        self.w = None
        self.r = {}
        self.gkey = gkey
        self.excl = False


class Sched:
    ENGS = ("sp", "pe", "act", "dve", "pool")

    def __init__(self, nc, stack):
        self.nc = nc
        self.stack = stack
        self.ops = {e: [] for e in self.ENGS}
        self.cnt = {e: 0 for e in self.ENGS}
        self.esem = {e: stack.enter_context(nc.semaphore("s_" + e)) for e in ("pe", "act", "dve", "pool")}
        self.seen = {e: {} for e in self.ENGS}
        self.groups = {}

    def group(self, key):
        if key not in self.groups:
            self.groups[key] = Grp(self.stack.enter_context(self.nc.semaphore("g_" + key)))
        return self.groups[key]

    def _waits(self, eng, reads, writes):
        need = {}

        def add(ev):
            if ev is None:
                return
            k = id(ev[0])
            if k not in need or need[k][1] < ev[1]:
                need[k] = ev

        for t in reads:
            add(t.w)
        for t in writes:
            add(t.w)
            for ev in t.r.values():
                add(ev)
        out = []
        for k, (sem, val) in need.items():
            if self.seen[eng].get(k, 0) >= val:
                continue
            if eng == "pe" and sem is self.esem["pe"]:
                continue
            self.seen[eng][k] = val
            out.append((sem, val))
        return out

    def _commit(self, ev, reads, writes):
        k = id(ev[0])
        for t in reads:
            if k not in t.r or t.r[k][1] < ev[1]:
                t.r[k] = ev
        for t in writes:
            t.w = ev
            t.r = {}

    def op(self, eng, reads, writes, fn):
        self.total = getattr(self, "total", 0) + 1
        if self.total > _OPLIMIT:
            return
        ex = [t for t in reads if t.excl]
        if ex:
            reads = [t for t in reads if not t.excl]
            writes = list(writes) + [t for t in ex if t not in writes]
        waits = self._waits(eng, reads, writes)
        self.cnt[eng] += 1
        ev = (self.esem[eng], self.cnt[eng])
        self.ops[eng].append((waits, fn, ev, 1))
        self._commit(ev, reads, writes)

    def dma(self, eng, dst, src, dst_ap, src_ap, by="dst", nowaw=False):
        self.total = getattr(self, "total", 0) + 1
        if self.total > _OPLIMIT:
            return
        saved = dst.w
        if nowaw and saved is not None and not any(saved[0] is es for es in self.esem.values()):
            dst.w = None
        waits = self._waits(eng, [src], [dst])
        dst.w = saved
        g = self.group(dst.gkey if by == "dst" else src.gkey)
        g.cum += 16
        ev = (g.sem, g.cum)
        self.ops[eng].append((waits, lambda e, o=dst_ap, i=src_ap: e.dma_start(out=o, in_=i), ev, 16))
        self._commit(ev, [src], [dst])

    def barrier(self):
        evs = [(self.esem[e], self.cnt[e]) for e in self.esem if self.cnt[e] > 0]
        evs += [(g.sem, g.cum) for g in self.groups.values() if g.cum > 0]
        for e in self.ENGS:
            w = []
            for sem, val in evs:
                k = id(sem)
                if self.seen[e].get(k, 0) >= val:
                    continue
                self.seen[e][k] = val
                w.append((sem, val))
            if w:
                self.ops[e].append((w, None, None, 0))

    def emit(self, block):
        decos = {"sp": block.sync, "pe": block.tensor, "act": block.scalar, "dve": block.vector, "pool": block.gpsimd}
        for e in self.ENGS:
            ops = self.ops[e]

            def body(engine, ops=ops):
                for waits, fn, ev, inc in ops:
                    for sem, val in waits:
                        engine.wait_ge(sem, val)
                    if fn is not None:
                        ins = fn(engine)
                        ins.then_inc(ev[0], inc)

            decos[e](body)


def _consts():
    c = {}
    bf = ml_dtypes.bfloat16
    idx = np.arange(128)
    c["c_ident"] = np.eye(128, dtype=np.float32).astype(bf)
    c["c_tril"] = np.where(idx[:, None] <= idx[None, :], -1.0 / 16.0, 0.0).astype(np.float32).astype(bf)
    c["c_triu"] = np.where(idx[:, None] > idx[None, :], -1.0 / 16.0, 0.0).astype(np.float32).astype(bf)
    c["c_mask"] = np.where(idx[:, None] <= idx[None, :], 1.0, 0.0).astype(np.float32)
    lg = np.log1p(-np.exp2(-5.0 - np.arange(RET_H, dtype=np.float64)))
    dpos = (idx[None, :] - idx[:, None]).astype(np.float64)
    retD = np.where(dpos[None] >= 0, np.exp(lg[:, None, None] * np.maximum(dpos[None], 0.0)), 0.0) * (RET_DK ** -0.5)
    c["c_retD"] = retD.astype(np.float32)
    eq = np.exp(lg[:, None] * (idx[None, :] + 1.0))
    eq2 = np.concatenate([eq, eq], axis=1)
    c["c_retEq"] = np.broadcast_to(eq2[:, None, :], (RET_H, 128, 256)).astype(np.float32).copy()
    e2 = np.exp(lg[None, :] * (127.0 - idx[:, None])) * (RET_DK ** -0.5)
    c["c_retE2"] = e2.astype(np.float32)
    dec = np.exp(lg * 128.0)
    c["c_retdec"] = np.broadcast_to(dec[None, :], (128, RET_H)).astype(np.float32).copy()
    inv_freq = (np.float32(10000.0) ** (-(np.arange(128, dtype=np.float32) / np.float32(128.0)))).astype(np.float32)
    c["c_invfreq"] = inv_freq[:, None].copy()
    return c


_CONST_SPECS = {
    "c_ident": ([128, 128], BF16), "c_tril": ([128, 128], BF16), "c_triu": ([128, 128], BF16),
    "c_mask": ([128, 128], F32), "c_retD": ([RET_H, 128, 128], F32), "c_retEq": ([RET_H, 128, 256], F32),
    "c_retE2": ([128, RET_H], F32), "c_retdec": ([128, RET_H], F32), "c_invfreq": ([128, 1], F32),
}

_INPUT_SPECS = {
    "gla_norm": [1, D], "gla_w_in": [1, D, 2 * 1024 + 2 * DI], "gla_w_a1": [1, D, 16], "gla_w_a2": [1, 16, 1024],
    "gla_b_a": [1, 1024], "gla_head_g": [1, DI], "gla_w_out": [1, DI, D], "ret_norm": [1, D],
    "ret_w_in": [1, D, 2 * D + 2 * DI], "ret_gn_g": [1, DI], "ret_gn_b": [1, DI], "ret_w_out": [1, DI, D],
    "final_norm": [1, D],
}


def build_program(NCH=32, layers=(0, 1), debug=False):
    nc = bass.Bass("TRN2", target_bir_lowering=False)
    S = NCH * 128
    dr = {}
    dr["x"] = nc.dram_tensor("x", [S, D], F32, kind="ExternalInput").ap()
    dr["positions"] = nc.dram_tensor("positions", [1, S], I32, kind="ExternalInput").ap()
    for k, shp in _INPUT_SPECS.items():
        dr[k] = nc.dram_tensor(k, shp, F32, kind="ExternalInput").ap()
    for k, (shp, dt) in _CONST_SPECS.items():
        dr[k] = nc.dram_tensor(k, shp, dt, kind="ExternalInput").ap()
    skind = "ExternalOutput" if debug else "Internal"
    dr["y"] = nc.dram_tensor("y", [S, D], F32, kind="ExternalOutput").ap()
    dr["hnT0"] = nc.dram_tensor("hnT0", [NCH, 128, D], BF16, kind=skind).ap()
    dr["hnT1"] = nc.dram_tensor("hnT1", [NCH, 128, D], BF16, kind=skind).ap()
    dr["og0"] = nc.dram_tensor("og0", [NCH, 128, DI], BF16, kind=skind).ap()
    dr["og1"] = nc.dram_tensor("og1", [NCH, 128, DI], BF16, kind=skind).ap()
    dr["h1"] = nc.dram_tensor("h1", [S, D], F32, kind=skind).ap()

    stack = ExitStack()
    sch = Sched(nc, stack)

    total = nc.sbuf_bytes_remaining
    arena = nc.alloc_sbuf_tensor("arena", [128, total - 64], U8)
    base = nc.lookup_mloc(arena).addr
    cur = [base]
    uid = [0]
    limit = base + total - 64

    def alloc(name, shape, dt, key=None):
        nb = int(np.prod(shape[1:])) * (4 if dt in (F32, I32) else 2)
        nb = (nb + 63) // 64 * 64
        assert cur[0] + nb <= limit, f"SBUF overflow at {name}: {cur[0] + nb - base} > {limit - base}"
        uid[0] += 1
        h = nc.alloc_sbuf_tensor_at(f"{name}_u{uid[0]}", list(shape), dt, offset=cur[0])
        cur[0] += nb
        return Tile(name, h, key or name)

    def mark():
        return cur[0]

    def release(m):
        cur[0] = m

    def dtile(name, ap, key):
        return Tile(name, ap, key)

    T_in = {k: dtile(k, dr[k], "in_" + k) for k in list(_INPUT_SPECS) + list(_CONST_SPECS) + ["positions"]}
    T_x = [dtile(f"x{c}", dr["x"][c * 128:(c + 1) * 128, :], "in_x") for c in range(NCH)]
    T_y = [dtile(f"y{c}", dr["y"][c * 128:(c + 1) * 128, :], "out_y") for c in range(NCH)]
    T_h1 = [dtile(f"h1_{c}", dr["h1"][c * 128:(c + 1) * 128, :], "d_h1") for c in range(NCH)]
    T_hnT = [[dtile(f"hnT{l}_{c}", dr[f"hnT{l}"][c], f"d_hnT{l}") for c in range(NCH)] for l in range(2)]
    T_og = [[dtile(f"og{l}_{c}", dr[f"og{l}"][c], f"d_og{l}") for c in range(NCH)] for l in range(2)]

    big = [Tile(f"pbig{i}", nc.alloc_psum_tensor(f"pbig{i}", [128, 1024], F32)) for i in range(2)]
    small = [Tile(f"psm{i}", nc.alloc_psum_tensor(f"psm{i}", [128, 512], F32)) for i in range(4)]
    for t in big + small:
        t.excl = True
    rr = {"big": 0, "small": 0}

    def pbig():
        t = big[rr["big"] % 2]
        rr["big"] += 1
        return t

    def psmall():
        t = small[rr["small"] % 4]
        rr["small"] += 1
        return t

    def load(eng, dst, src, dst_ap=None, src_ap=None):
        sch.dma(eng, dst, src, dst.ap[:] if dst_ap is None else dst_ap, src.ap if src_ap is None else src_ap)

    def store(eng, dst, src, dst_ap, src_ap):
        sch.dma(eng, dst, src, dst_ap, src_ap, by="src", nowaw=True)

    ident = alloc("ident", [128, 128], BF16)
    tril = alloc("tril", [128, 128], BF16)
    triu = alloc("triu", [128, 128], BF16)
    mask = alloc("mask", [128, 128], F32)
    retE2 = alloc("retE2", [128, RET_H], F32)
    retdec = alloc("retdec", [128, RET_H], F32)
    invfreq = alloc("invfreq", [128, 1], F32)
    eps_t = alloc("eps_t", [128, 1], F32)
    one_t = alloc("one_t", [128, 1], F32)
    for t, k in ((ident, "c_ident"), (tril, "c_tril"), (triu, "c_triu"), (mask, "c_mask"), (retE2, "c_retE2"),
                 (retdec, "c_retdec"), (invfreq, "c_invfreq")):
        load("sp", t, T_in[k])
    sch.op("dve", [], [eps_t], lambda e: e.memset(eps_t.ap[:], EPS))
    sch.op("dve", [], [one_t], lambda e: e.memset(one_t.ap[:], 1.0))

    def rstd_from_ss(ss, lnv, rstd, n, extra_reads=()):
        sch.op("act", [ss, eps_t] + list(extra_reads), [lnv],
               lambda e: e.activation(out=lnv.ap[:], in_=ss.ap[:], func=AF.Ln, scale=1.0 / n, bias=eps_t.ap[:, 0:1]))
        sch.op("act", [lnv], [rstd], lambda e: e.activation(out=rstd.ap[:], in_=lnv.ap[:], func=AF.Exp, scale=-0.5))

    class NormBufs:
        pass

    def alloc_norm_bufs(with_r):
        nb = NormBufs()
        nb.junk = alloc("junk", [128, D], BF16)
        nb.hn = alloc("hn", [128, D], BF16)
        nb.hnT = alloc("hnT_sb", [128, D], BF16)
        nb.ss = alloc("n_ss", [128, 1], F32)
        nb.lnv = alloc("n_lnv", [128, 1], F32)
        nb.rstd = alloc("n_rstd", [128, 1], F32)
        nb.gbc = alloc("gbc", [128, D], F32)
        return nb

    def norm_stats(nb, src):
        sch.op("act", [src], [nb.junk, nb.ss],
               lambda e: e.activation(out=nb.junk.ap[:], in_=src.ap[:], func=AF.Square, accum_out=nb.ss.ap[:, 0:1]))
        rstd_from_ss(nb.ss, nb.lnv, nb.rstd, D)

    def norm_and_transpose(nb, c, src, dst_hnT, r_all=None, w_a1p=None):
        norm_stats(nb, src)
        sch.op("dve", [src, nb.rstd, nb.gbc], [nb.hn],
               lambda e: e.scalar_tensor_tensor(out=nb.hn.ap[:], in0=src.ap[:], scalar=nb.rstd.ap[:, 0:1],
                                                in1=nb.gbc.ap[:], op0=ALU.mult, op1=ALU.mult))
        pt = pbig()
        ptb = pt.ap[:].bitcast(BF16)

        def tr(e):
            ins = None
            for kc in range(KC):
                ins = e.transpose(out=ptb[:, kc * 128:(kc + 1) * 128], in_=nb.hn.ap[:, kc * 128:(kc + 1) * 128],
                                  identity=ident.ap[:])
            return ins

        sch.op("pe", [nb.hn, ident], [pt], tr)
        sch.op("act", [pt], [nb.hnT], lambda e: e.activation(out=nb.hnT.ap[:, 0:1024], in_=ptb[:, 0:1024], func=AF.Copy))
        sch.op("dve", [pt], [nb.hnT], lambda e: e.tensor_copy(out=nb.hnT.ap[:, 1024:2048], in_=ptb[:, 1024:2048]))
        store("sp", dst_hnT, nb.hnT, dst_hnT.ap, nb.hnT.ap[:])
        if r_all is not None:
            pr = psmall()

            def rmm(e):
                ins = None
                for kc in range(KC):
                    ins = e.matmul(pr.ap[:, 0:128], lhsT=w_a1p.ap[:, kc, :], rhs=nb.hnT.ap[:, kc * 128:(kc + 1) * 128],
                                   start=(kc == 0), stop=(kc == KC - 1))
                return ins

            sch.op("pe", [w_a1p, nb.hnT], [pr], rmm)
            sch.op("act", [pr], [r_all],
                   lambda e: e.activation(out=r_all.ap[0:32, c * 128:(c + 1) * 128], in_=pr.ap[0:32, 0:128], func=AF.Copy))

    def load_w(dst, src_tile, src_cols, ncols, npieces=4):
        srcv = src_tile.ap[0].rearrange("(kc p) f -> p kc f", p=128)
        for i in range(KC):
            sch.dma("pool", dst, src_tile, dst.ap[:, i, :], srcv[:, i, src_cols:src_cols + ncols], nowaw=(i > 0))

    def phase_C(layer, w_out_key, res_tiles, next_gain_key, final):
        m = mark()
        wout = alloc("wout", [128, DI // 128, D], BF16)
        srcv = T_in[w_out_key].ap[0].rearrange("(fc p) d -> p fc d", p=128)
        for i in range(DI // 128):
            sch.dma("pool", wout, T_in[w_out_key], wout.ap[:, i, :], srcv[:, i, :], nowaw=(i > 0))
        nb = alloc_norm_bufs(False)
        load("sp", nb.gbc, T_in[next_gain_key], nb.gbc.ap[:], T_in[next_gain_key].ap.broadcast_to([128, D]))
        xs = [alloc(f"xs{i}", [128, D], F32) for i in range(2)]
        ogs = [alloc(f"ogc{i}", [128, DI], BF16) for i in range(2)]
        yo = alloc("yo", [128, D], F32) if final else None
        for c in range(NCH):
            x_t = xs[c % 2]
            og_t = ogs[c % 2]
            load("sp", og_t, T_og[layer][c])
            load("sp", x_t, res_tiles[c])
            for nbk in range(4):
                py = psmall()

                def mm(e, py=py, nbk=nbk, og_t=og_t):
                    ins = None
                    for fc in range(DI // 128):
                        ins = e.matmul(py.ap[:, :], lhsT=og_t.ap[:, fc * 128:(fc + 1) * 128],
                                       rhs=wout.ap[:, fc, nbk * 512:(nbk + 1) * 512], start=(fc == 0), stop=(fc == DI // 128 - 1))
                    return ins

                sch.op("pe", [og_t, wout], [py], mm)
                sch.op("dve", [py, x_t], [x_t],
                       lambda e, py=py, nbk=nbk, x_t=x_t: e.tensor_tensor(out=x_t.ap[:, nbk * 512:(nbk + 1) * 512],
                                                                          in0=py.ap[:, :], in1=x_t.ap[:, nbk * 512:(nbk + 1) * 512],
                                                                          op=ALU.add))
            if not final:
                store("sp", T_h1[c], x_t, T_h1[c].ap, x_t.ap[:])
                norm_and_transpose(nb, c, x_t, T_hnT[1][c])
            else:
                norm_stats(nb, x_t)
                sch.op("dve", [x_t, nb.rstd, nb.gbc], [yo],
                       lambda e, x_t=x_t: e.scalar_tensor_tensor(out=yo.ap[:], in0=x_t.ap[:], scalar=nb.rstd.ap[:, 0:1],
                                                                 in1=nb.gbc.ap[:], op0=ALU.mult, op1=ALU.mult))
                store("sp", T_y[c], yo, T_y[c].ap, yo.ap[:])
        sch.barrier()
        release(m)

    def inproj_featmajor(pq, wq, wk, hT):
        def mm(e):
            ins = None
            for j, (w, jj) in enumerate(((wq, 0), (wq, 1), (wk, 0), (wk, 1))):
                for kc in range(KC):
                    ins = e.matmul(pq.ap[:, j * 128:(j + 1) * 128], lhsT=w.ap[:, kc, jj * 128:(jj + 1) * 128],
                                   rhs=hT.ap[:, kc * 128:(kc + 1) * 128], start=(kc == 0), stop=(kc == KC - 1))
            return ins
        sch.op("pe", [wq, wk, hT], [pq], mm)

    def inproj_tokmajor(pt_, w, hT, col0):
        def mm(e):
            ins = None
            for kc in range(KC):
                ins = e.matmul(pt_.ap[:, :], lhsT=hT.ap[:, kc * 128:(kc + 1) * 128], rhs=w.ap[:, kc, col0:col0 + 512],
                               start=(kc == 0), stop=(kc == KC - 1))
            return ins
        sch.op("pe", [w, hT], [pt_], mm)

    def layer_gla():
        mL = mark()
        r_all = alloc("r_all", [128, S], BF16)
        w_a1p = alloc("w_a1p", [128, KC, 128], BF16)
        w_a2a = alloc("w_a2a", [128, 1024], BF16)
        sch.op("dve", [], [r_all], lambda e: e.memset(r_all.ap[:], 0.0))
        sch.op("dve", [r_all], [r_all], lambda e: e.memset(r_all.ap[32:33, :], 1.0))
        sch.op("pool", [], [w_a1p], lambda e: e.memset(w_a1p.ap[:], 0.0))
        sch.op("pool", [], [w_a2a], lambda e: e.memset(w_a2a.ap[:], 0.0))
        a1v = T_in["gla_w_a1"].ap[0].rearrange("(kc p) j -> p kc j", p=128)
        for kc in range(KC):
            sch.dma("pool", w_a1p, T_in["gla_w_a1"], w_a1p.ap[:, kc, 0:16], a1v[:, kc, :], nowaw=(kc > 0))
        sch.dma("pool", w_a2a, T_in["gla_w_a2"], w_a2a.ap[0:16, :], T_in["gla_w_a2"].ap[0])
        sch.dma("pool", w_a2a, T_in["gla_b_a"], w_a2a.ap[32:33, :], T_in["gla_b_a"].ap, nowaw=True)
        wq = alloc("g_wq", [128, KC, 256], BF16)
        wk = alloc("g_wk", [128, KC, 256], BF16)
        wv = alloc("g_wv", [128, KC, 1024], BF16)
        wg = alloc("g_wg", [128, KC, 1024], BF16)

        def load_head(h):
            W = T_in["gla_w_in"]
            load_w(wq, W, h * 256, 256, 2)
            load_w(wk, W, 1024 + h * 256, 256, 2)
            load_w(wv, W, 2048 + h * 1024, 1024, 4)
            load_w(wg, W, 2048 + DI + h * 1024, 1024, 4)

        load_head(0)
        if _DBG_STAGE <= 1:
            sch.barrier(); release(mL); return
        mA = mark()
        nb = alloc_norm_bufs(True)
        load("sp", nb.gbc, T_in["gla_norm"], nb.gbc.ap[:], T_in["gla_norm"].ap.broadcast_to([128, D]))
        xs = [alloc(f"xsA{i}", [128, D], F32) for i in range(2)]
        for c in range(NCH):
            x_t = xs[c % 2]
            load("sp", x_t, T_x[c])
            norm_and_transpose(nb, c, x_t, T_hnT[0][c], r_all, w_a1p)
        sch.barrier()
        release(mA)
        if _DBG_STAGE <= 2:
            release(mL); return
        hT_ring = [alloc(f"hT{i}", [128, D], BF16) for i in range(3)]
        S32 = alloc("S32", [128, 2 * GLA_DV], F32)
        Sbf = [alloc(f"Sbf{i}", [128, 2 * GLA_DV], BF16) for i in range(2)]
        hg = alloc("hg", [128, GLA_DV], F32)

        class B:
            pass

        bufs = []
        for i in range(2):
            b = B()
            b.e = alloc(f"b_e{i}", [128, 256], F32)
            b.sp = alloc(f"b_sp{i}", [128, 256], F32)
            b.hi = alloc(f"b_hi{i}", [128, 256], BF16)
            b.lo = alloc(f"b_lo{i}", [128, 256], BF16)
            b.Eq = alloc(f"b_Eq{i}", [128, 256], F32)
            b.Ek = alloc(f"b_Ek{i}", [128, 256], F32)
            b.E2 = alloc(f"b_E2{i}", [128, 256], F32)
            b.qd = alloc(f"b_qd{i}", [128, 256], BF16)
            b.kd = alloc(f"b_kd{i}", [128, 256], BF16)
            b.kbf = alloc(f"b_kbf{i}", [128, 256], BF16)
            b.kd2 = alloc(f"b_kd2{i}", [128, 256], BF16)
            b.scT = alloc(f"b_scT{i}", [128, 128], BF16)
            b.v = alloc(f"b_v{i}", [128, GLA_DV], BF16)
            b.sg = alloc(f"b_sg{i}", [128, GLA_DV], F32)
            b.sgg = alloc(f"b_sgg{i}", [128, GLA_DV], F32)
            b.og = alloc(f"b_og{i}", [128, GLA_DV], BF16)
            b.ogT = alloc(f"b_ogT{i}", [128, GLA_DV], BF16)
            b.junk = alloc(f"b_junk{i}", [128, GLA_DV], BF16)
            b.ss = alloc(f"b_ss{i}", [128, 1], F32)
            b.lnv = alloc(f"b_lnv{i}", [128, 1], F32)
            b.rstd = alloc(f"b_rstd{i}", [128, 1], F32)
            bufs.append(b)
        it = 0
        for h in range(GLA_H):
            if h > 0:
                load_head(h)
            load("sp", hg, T_in["gla_head_g"], hg.ap[:],
                 T_in["gla_head_g"].ap[0:1, h * GLA_DV:(h + 1) * GLA_DV].broadcast_to([128, GLA_DV]))
            sch.op("pool", [], [S32], lambda e: e.memset(S32.ap[:], 0.0))
            sch.op("pool", [], [Sbf[0]], lambda e: e.memset(Sbf[0].ap[:], 0.0))
            for c in range(NCH):
                b = bufs[it % 2]
                hT = hT_ring[it % 3]
                Scur, Snxt = Sbf[c % 2], Sbf[(c + 1) % 2]
                it += 1
                load("sp", hT, T_hnT[0][c])
                pz = psmall()
                sch.op("pe", [r_all, w_a2a], [pz],
                       lambda e, pz=pz, c=c, h=h: e.matmul(pz.ap[:, 0:256], lhsT=r_all.ap[:, c * 128:(c + 1) * 128],
                                                            rhs=w_a2a.ap[:, h * 256:(h + 1) * 256], start=True, stop=True))
                sch.op("act", [pz], [b.e], lambda e, pz=pz, b=b: e.activation(out=b.e.ap[:], in_=pz.ap[:, 0:256], func=AF.Exp, scale=-1.0))
                sch.op("act", [b.e, one_t], [b.sp],
                       lambda e, b=b: e.activation(out=b.sp.ap[:], in_=b.e.ap[:], func=AF.Ln, bias=one_t.ap[:, 0:1]))
                sch.op("dve", [b.sp], [b.hi], lambda e, b=b: e.tensor_copy(out=b.hi.ap[:], in_=b.sp.ap[:]))
                sch.op("dve", [b.sp, b.hi], [b.lo],
                       lambda e, b=b: e.tensor_tensor(out=b.lo.ap[:], in0=b.sp.ap[:], in1=b.hi.ap[:], op=ALU.subtract))
                pc = psmall()

                def cums(e, pc=pc, b=b):
                    ins = None
                    for j in range(2):
                        e.matmul(pc.ap[:, j * 128:(j + 1) * 128], lhsT=b.hi.ap[:, j * 128:(j + 1) * 128], rhs=tril.ap[:], start=True, stop=False)
                        e.matmul(pc.ap[:, j * 128:(j + 1) * 128], lhsT=b.lo.ap[:, j * 128:(j + 1) * 128], rhs=tril.ap[:], start=False, stop=True)
                    e.matmul(pc.ap[:, 256:512], lhsT=triu.ap[:], rhs=b.hi.ap[:], start=True, stop=False)
                    ins = e.matmul(pc.ap[:, 256:512], lhsT=triu.ap[:], rhs=b.lo.ap[:], start=False, stop=True)
                    return ins

                sch.op("pe", [b.hi, b.lo, tril, triu], [pc], cums)
                sch.op("act", [pc], [b.Eq], lambda e, pc=pc, b=b: e.activation(out=b.Eq.ap[:], in_=pc.ap[:, 0:256], func=AF.Exp))
                sch.op("act", [pc], [b.Ek], lambda e, pc=pc, b=b: e.activation(out=b.Ek.ap[:], in_=pc.ap[:, 0:256], func=AF.Exp, scale=-1.0))
                sch.op("act", [pc], [b.E2], lambda e, pc=pc, b=b: e.activation(out=b.E2.ap[:], in_=pc.ap[:, 256:512], func=AF.Exp))
                pq = psmall()
                inproj_featmajor(pq, wq, wk, hT)
                sch.op("dve", [pq, b.Eq], [b.qd],
                       lambda e, pq=pq, b=b: e.scalar_tensor_tensor(out=b.qd.ap[:], in0=pq.ap[:, 0:256], scalar=GLA_DK ** -0.5,
                                                                    in1=b.Eq.ap[:], op0=ALU.mult, op1=ALU.mult))
                sch.op("dve", [pq, b.Ek], [b.kd],
                       lambda e, pq=pq, b=b: e.tensor_tensor(out=b.kd.ap[:], in0=pq.ap[:, 256:512], in1=b.Ek.ap[:], op=ALU.mult))
                sch.op("act", [pq], [b.kbf], lambda e, pq=pq, b=b: e.activation(out=b.kbf.ap[:], in_=pq.ap[:, 256:512], func=AF.Copy))
                for nbk in range(2):
                    pv = psmall()
                    inproj_tokmajor(pv, wv, hT, nbk * 512)
                    sch.op("act", [pv], [b.v],
                           lambda e, pv=pv, b=b, nbk=nbk: e.activation(out=b.v.ap[:, nbk * 512:(nbk + 1) * 512], in_=pv.ap[:, :], func=AF.Copy))
                for nbk in range(2):
                    pg = psmall()
                    inproj_tokmajor(pg, wg, hT, nbk * 512)
                    sch.op("act", [pg], [b.sg],
                           lambda e, pg=pg, b=b, nbk=nbk: e.activation(out=b.sg.ap[:, nbk * 512:(nbk + 1) * 512], in_=pg.ap[:, :], func=AF.Silu))
                sch.op("pool", [b.sg, hg], [b.sgg], lambda e, b=b: e.tensor_tensor(out=b.sgg.ap[:], in0=b.sg.ap[:], in1=hg.ap[:], op=ALU.mult))
                pk = psmall()
                pkb = pk.ap[:].bitcast(BF16)

                def ktr(e, pkb=pkb, b=b):
                    e.transpose(out=pkb[:, 0:128], in_=b.kbf.ap[:, 0:128], identity=ident.ap[:])
                    return e.transpose(out=pkb[:, 128:256], in_=b.kbf.ap[:, 128:256], identity=ident.ap[:])

                sch.op("pe", [b.kbf, ident], [pk], ktr)
                sch.op("dve", [pk, b.E2], [b.kd2],
                       lambda e, pkb=pkb, b=b: e.tensor_tensor(out=b.kd2.ap[:], in0=pkb[:, 0:256], in1=b.E2.ap[:], op=ALU.mult))
                psc = psmall()

                def scm(e, psc=psc, b=b):
                    e.matmul(psc.ap[:, 0:128], lhsT=b.kd.ap[:, 0:128], rhs=b.qd.ap[:, 0:128], start=True, stop=False)
                    return e.matmul(psc.ap[:, 0:128], lhsT=b.kd.ap[:, 128:256], rhs=b.qd.ap[:, 128:256], start=False, stop=True)

                sch.op("pe", [b.kd, b.qd], [psc], scm)
                sch.op("dve", [psc, mask], [b.scT],
                       lambda e, psc=psc, b=b: e.tensor_tensor(out=b.scT.ap[:], in0=psc.ap[:, 0:128], in1=mask.ap[:], op=ALU.mult))
                po = pbig()

                def omm(e, po=po, b=b, Scur=Scur):
                    ins = None
                    for nbk in range(2):
                        sl = slice(nbk * 512, (nbk + 1) * 512)
                        e.matmul(po.ap[:, sl], lhsT=b.scT.ap[:], rhs=b.v.ap[:, sl], start=True, stop=False)
                        e.matmul(po.ap[:, sl], lhsT=b.qd.ap[:, 0:128], rhs=Scur.ap[:, nbk * 512:(nbk + 1) * 512], start=False, stop=False)
                        ins = e.matmul(po.ap[:, sl], lhsT=b.qd.ap[:, 128:256], rhs=Scur.ap[:, GLA_DV + nbk * 512:GLA_DV + (nbk + 1) * 512], start=False, stop=True)
                    return ins

                sch.op("pe", [b.scT, b.v, b.qd, Scur], [po], omm)
                sch.op("act", [po], [b.junk, b.ss],
                       lambda e, po=po, b=b: e.activation(out=b.junk.ap[:], in_=po.ap[:, :], func=AF.Square, accum_out=b.ss.ap[:, 0:1]))
                rstd_from_ss(b.ss, b.lnv, b.rstd, GLA_DV)
                sch.op("dve", [po, b.rstd, b.sgg], [b.og],
                       lambda e, po=po, b=b: e.scalar_tensor_tensor(out=b.og.ap[:], in0=po.ap[:, :], scalar=b.rstd.ap[:, 0:1],
                                                                    in1=b.sgg.ap[:], op0=ALU.mult, op1=ALU.mult))
                for j in range(2):
                    pu = pbig()

                    def umm(e, pu=pu, b=b, j=j):
                        e.matmul(pu.ap[:, 0:512], lhsT=b.kd2.ap[:, j * 128:(j + 1) * 128], rhs=b.v.ap[:, 0:512], start=True, stop=True)
                        return e.matmul(pu.ap[:, 512:1024], lhsT=b.kd2.ap[:, j * 128:(j + 1) * 128], rhs=b.v.ap[:, 512:1024], start=True, stop=True)

                    sch.op("pe", [b.kd2, b.v], [pu], umm)
                    sch.op("dve", [pu, b.Eq, S32], [S32],
                           lambda e, pu=pu, b=b, j=j: e.scalar_tensor_tensor(out=S32.ap[:, j * GLA_DV:(j + 1) * GLA_DV], in0=S32.ap[:, j * GLA_DV:(j + 1) * GLA_DV],
                                                                             scalar=b.Eq.ap[:, j * 128 + 127:j * 128 + 128],
                                                                             in1=pu.ap[:, :], op0=ALU.mult, op1=ALU.add))
                sch.op("pool", [S32], [Snxt], lambda e, Snxt=Snxt: e.tensor_copy(out=Snxt.ap[:], in_=S32.ap[:]))
                pg2 = psmall()
                pg2b = pg2.ap[:].bitcast(BF16)

                def otr(e, pg2b=pg2b, b=b):
                    ins = None
                    for i in range(GLA_DV // 128):
                        ins = e.transpose(out=pg2b[:, i * 128:(i + 1) * 128], in_=b.og.ap[:, i * 128:(i + 1) * 128], identity=ident.ap[:])
                    return ins

                sch.op("pe", [b.og, ident], [pg2], otr)
                sch.op("act", [pg2], [b.ogT], lambda e, pg2b=pg2b, b=b: e.activation(out=b.ogT.ap[:], in_=pg2b[:, 0:GLA_DV], func=AF.Copy))
                store("sp", T_og[0][c], b.ogT, T_og[0][c].ap[:, h * GLA_DV:(h + 1) * GLA_DV], b.ogT.ap[:])
        sch.barrier()
        release(mL)

    def layer_ret():
        mL = mark()
        cosT = alloc("cosT", [128, S], F32)
        sinT = alloc("sinT", [128, S], F32)
        W = T_in["ret_w_in"]
        slots = []
        for i in range(2):
            slots.append((alloc(f"r_wq{i}", [128, KC, 256], BF16), alloc(f"r_wk{i}", [128, KC, 256], BF16),
                          alloc(f"r_wv{i}", [128, KC, RET_DV], BF16), alloc(f"r_wg{i}", [128, KC, RET_DV], BF16),
                          alloc(f"r_D{i}", [128, 128], F32), alloc(f"r_Eq{i}", [128, 256], F32),
                          alloc(f"r_gg{i}", [128, RET_DV], F32), alloc(f"r_gb{i}", [128, RET_DV], F32)))

        def load_head(h):
            wq, wk, wv, wg, Dh, Eqh, gg, gb = slots[h % 2]
            load_w(wq, W, h * 256, 256, 2)
            load_w(wk, W, D + h * 256, 256, 2)
            load_w(wv, W, 2 * D + h * RET_DV, RET_DV, 4)
            load_w(wg, W, 2 * D + DI + h * RET_DV, RET_DV, 4)
            load("sp", Dh, T_in["c_retD"], Dh.ap[:], T_in["c_retD"].ap[h])
            load("sp", Eqh, T_in["c_retEq"], Eqh.ap[:], T_in["c_retEq"].ap[h])
            load("sp", gg, T_in["ret_gn_g"], gg.ap[:], T_in["ret_gn_g"].ap[0:1, h * RET_DV:(h + 1) * RET_DV].broadcast_to([128, RET_DV]))
            load("sp", gb, T_in["ret_gn_b"], gb.ap[:], T_in["ret_gn_b"].ap[0:1, h * RET_DV:(h + 1) * RET_DV].broadcast_to([128, RET_DV]))

        load_head(0)
        mT = mark()
        PW = 512
        posi = alloc("posi", [128, PW], I32)
        posf = alloc("posf", [128, PW], F32)
        yv = alloc("yv", [128, PW], F32)
        yi = alloc("yi", [128, PW], I32)
        yf = alloc("yf", [128, PW], F32)
        for p0 in range(0, S, PW):
            w_ = min(PW, S - p0)
            load("sp", posi, T_in["positions"], posi.ap[:, 0:w_], T_in["positions"].ap[0:1, p0:p0 + w_].broadcast_to([128, w_]))
            sch.op("dve", [posi], [posf], lambda e, w_=w_: e.tensor_copy(out=posf.ap[:, 0:w_], in_=posi.ap[:, 0:w_]))
            for tab, shift in ((sinT, 0.0), (cosT, 0.25)):
                sch.op("dve", [posf, invfreq], [yv],
                       lambda e, w_=w_: e.tensor_scalar(out=yv.ap[:, 0:w_], in0=posf.ap[:, 0:w_], scalar1=invfreq.ap[:, 0:1],
                                                        scalar2=None, op0=ALU.mult))
                sch.op("dve", [yv], [yv],
                       lambda e, w_=w_, shift=shift: e.tensor_scalar(out=yv.ap[:, 0:w_], in0=yv.ap[:, 0:w_], scalar1=1.0 / (2 * math.pi),
                                                                     scalar2=shift, op0=ALU.mult, op1=ALU.add))
                sch.op("dve", [yv], [yi], lambda e, w_=w_: e.tensor_copy(out=yi.ap[:, 0:w_], in_=yv.ap[:, 0:w_]))
                sch.op("dve", [yi], [yf], lambda e, w_=w_: e.tensor_copy(out=yf.ap[:, 0:w_], in_=yi.ap[:, 0:w_]))
                sch.op("dve", [yv, yf], [yv],
                       lambda e, w_=w_: e.tensor_tensor(out=yv.ap[:, 0:w_], in0=yv.ap[:, 0:w_], in1=yf.ap[:, 0:w_], op=ALU.subtract))
                sch.op("dve", [yv], [yf],
                       lambda e, w_=w_: e.tensor_single_scalar(out=yf.ap[:, 0:w_], in_=yv.ap[:, 0:w_], scalar=0.5, op=ALU.is_gt))
                sch.op("dve", [yv, yf], [yv],
                       lambda e, w_=w_: e.tensor_tensor(out=yv.ap[:, 0:w_], in0=yv.ap[:, 0:w_], in1=yf.ap[:, 0:w_], op=ALU.subtract))
                sch.op("dve", [yv], [yf],
                       lambda e, w_=w_: e.tensor_single_scalar(out=yf.ap[:, 0:w_], in_=yv.ap[:, 0:w_], scalar=-0.5, op=ALU.is_lt))
                sch.op("dve", [yv, yf], [yv],
                       lambda e, w_=w_: e.tensor_tensor(out=yv.ap[:, 0:w_], in0=yv.ap[:, 0:w_], in1=yf.ap[:, 0:w_], op=ALU.add))
                sch.op("act", [yv], [tab],
                       lambda e, w_=w_, tab=tab, p0=p0: e.activation(out=tab.ap[:, p0:p0 + w_], in_=yv.ap[:, 0:w_], func=AF.Sin,
                                                                      scale=2 * math.pi * 0.999999))
        sch.barrier()
        release(mT)
        hT_ring = [alloc(f"rhT{i}", [128, D], BF16) for i in range(3)]
        S32 = alloc("rS32", [128, 2 * RET_DV], F32)
        Sbf = [alloc(f"rSbf{i}", [128, 2 * RET_DV], BF16) for i in range(2)]

        class B:
            pass

        bufs = []
        for i in range(2):
            b = B()
            b.tc = alloc(f"rb_tc{i}", [128, 512], F32)
            b.ts = alloc(f"rb_ts{i}", [128, 512], F32)
            b.rq = alloc(f"rb_rq{i}", [128, 256], F32)
            b.qr = alloc(f"rb_qr{i}", [128, 256], BF16)
            b.qd = alloc(f"rb_qd{i}", [128, 256], BF16)
            b.kr = alloc(f"rb_kr{i}", [128, 256], BF16)
            b.kd2 = alloc(f"rb_kd2{i}", [128, 256], BF16)
            b.scT = alloc(f"rb_scT{i}", [128, 128], BF16)
            b.v = alloc(f"rb_v{i}", [128, RET_DV], BF16)
            b.sg = alloc(f"rb_sg{i}", [128, RET_DV], F32)
            b.xn = alloc(f"rb_xn{i}", [128, RET_DV], F32)
            b.og = alloc(f"rb_og{i}", [128, RET_DV], BF16)
            b.ogT = alloc(f"rb_ogT{i}", [128, RET_DV], BF16)
            b.junk = alloc(f"rb_junk{i}", [128, RET_DV], BF16)
            b.st = alloc(f"rb_st{i}", [128, 8], F32)
            bufs.append(b)
        it = 0
        for h in range(RET_H):
            wq, wk, wv, wg, Dh, Eqh, gg, gb = slots[h % 2]
            if h + 1 < RET_H:
                load_head(h + 1)
            sch.op("pool", [], [S32], lambda e: e.memset(S32.ap[:], 0.0))
            sch.op("pool", [], [Sbf[0]], lambda e: e.memset(Sbf[0].ap[:], 0.0))
            for c in range(NCH):
                b = bufs[it % 2]
                hT = hT_ring[it % 3]
                Scur, Snxt = Sbf[c % 2], Sbf[(c + 1) % 2]
                it += 1
                load("sp", hT, T_hnT[1][c])
                pq = psmall()
                inproj_featmajor(pq, wq, wk, hT)
                cs = slice(c * 128, (c + 1) * 128)
                pq4 = pq.ap[:, :].rearrange("p (j t) -> p j t", t=128)
                sch.op("dve", [pq, cosT], [b.tc],
                       lambda e, pq4=pq4, b=b, cs=cs: e.tensor_tensor(out=b.tc.ap[:, :].rearrange("p (j t) -> p j t", t=128), in0=pq4,
                                                                      in1=cosT.ap[:, cs].unsqueeze(1).broadcast_to([128, 4, 128]), op=ALU.mult))
                sch.op("dve", [pq, sinT], [b.ts],
                       lambda e, pq4=pq4, b=b, cs=cs: e.tensor_tensor(out=b.ts.ap[:, :].rearrange("p (j t) -> p j t", t=128), in0=pq4,
                                                                      in1=sinT.ap[:, cs].unsqueeze(1).broadcast_to([128, 4, 128]), op=ALU.mult))
                sch.op("pool", [b.tc, b.ts], [b.rq],
                       lambda e, b=b: e.tensor_tensor(out=b.rq.ap[:, 0:128], in0=b.tc.ap[:, 0:128], in1=b.ts.ap[:, 128:256], op=ALU.subtract))
                sch.op("pool", [b.tc, b.ts], [b.rq],
                       lambda e, b=b: e.tensor_tensor(out=b.rq.ap[:, 128:256], in0=b.tc.ap[:, 128:256], in1=b.ts.ap[:, 0:128], op=ALU.add))
                sch.op("pool", [b.tc, b.ts], [b.kr],
                       lambda e, b=b: e.tensor_tensor(out=b.kr.ap[:, 0:128], in0=b.tc.ap[:, 256:384], in1=b.ts.ap[:, 384:512], op=ALU.subtract))
                sch.op("pool", [b.tc, b.ts], [b.kr],
                       lambda e, b=b: e.tensor_tensor(out=b.kr.ap[:, 128:256], in0=b.tc.ap[:, 384:512], in1=b.ts.ap[:, 256:384], op=ALU.add))
                sch.op("pool", [b.rq], [b.qr], lambda e, b=b: e.tensor_copy(out=b.qr.ap[:], in_=b.rq.ap[:]))
                sch.op("dve", [b.rq, Eqh], [b.qd],
                       lambda e, b=b, Eqh=Eqh: e.tensor_tensor(out=b.qd.ap[:], in0=b.rq.ap[:], in1=Eqh.ap[:], op=ALU.mult))
                pv = psmall()
                inproj_tokmajor(pv, wv, hT, 0)
                sch.op("act", [pv], [b.v], lambda e, pv=pv, b=b: e.activation(out=b.v.ap[:], in_=pv.ap[:, :], func=AF.Copy))
                pg = psmall()
                inproj_tokmajor(pg, wg, hT, 0)
                sch.op("act", [pg], [b.sg], lambda e, pg=pg, b=b: e.activation(out=b.sg.ap[:], in_=pg.ap[:, :], func=AF.Silu))
                pk = psmall()
                pkb = pk.ap[:].bitcast(BF16)

                def ktr(e, pkb=pkb, b=b):
                    e.transpose(out=pkb[:, 0:128], in_=b.kr.ap[:, 0:128], identity=ident.ap[:])
                    return e.transpose(out=pkb[:, 128:256], in_=b.kr.ap[:, 128:256], identity=ident.ap[:])

                sch.op("pe", [b.kr, ident], [pk], ktr)
                sch.op("dve", [pk, retE2], [b.kd2],
                       lambda e, pkb=pkb, b=b, h=h: e.tensor_scalar(out=b.kd2.ap[:], in0=pkb[:, 0:256], scalar1=retE2.ap[:, h:h + 1],
                                                                    scalar2=None, op0=ALU.mult))
                psc = psmall()

                def scm(e, psc=psc, b=b):
                    e.matmul(psc.ap[:, 0:128], lhsT=b.kr.ap[:, 0:128], rhs=b.qr.ap[:, 0:128], start=True, stop=False)
                    return e.matmul(psc.ap[:, 0:128], lhsT=b.kr.ap[:, 128:256], rhs=b.qr.ap[:, 128:256], start=False, stop=True)

                sch.op("pe", [b.kr, b.qr], [psc], scm)
                sch.op("dve", [psc, Dh], [b.scT],
                       lambda e, psc=psc, b=b, Dh=Dh: e.tensor_tensor(out=b.scT.ap[:], in0=psc.ap[:, 0:128], in1=Dh.ap[:], op=ALU.mult))
                po = psmall()

                def omm(e, po=po, b=b, Scur=Scur):
                    e.matmul(po.ap[:, :], lhsT=b.scT.ap[:], rhs=b.v.ap[:], start=True, stop=False)
                    e.matmul(po.ap[:, :], lhsT=b.qd.ap[:, 0:128], rhs=Scur.ap[:, 0:RET_DV], start=False, stop=False)
                    return e.matmul(po.ap[:, :], lhsT=b.qd.ap[:, 128:256], rhs=Scur.ap[:, RET_DV:2 * RET_DV], start=False, stop=True)

                sch.op("pe", [b.scT, b.v, b.qd, Scur], [po], omm)
                st = b.st
                sch.op("act", [po], [b.junk, st],
                       lambda e, po=po, b=b: e.activation(out=b.junk.ap[:], in_=po.ap[:, :], func=AF.Copy, accum_out=b.st.ap[:, 0:1]))
                sch.op("act", [po, st], [b.junk, st],
                       lambda e, po=po, b=b: e.activation(out=b.junk.ap[:], in_=po.ap[:, :], func=AF.Square, accum_out=b.st.ap[:, 1:2]))
                sch.op("dve", [st], [st],
                       lambda e, b=b: e.tensor_scalar(out=b.st.ap[:, 2:3], in0=b.st.ap[:, 0:1], scalar1=-1.0 / RET_DV, scalar2=None, op0=ALU.mult))
                sch.op("dve", [st], [st],
                       lambda e, b=b: e.tensor_tensor(out=b.st.ap[:, 3:4], in0=b.st.ap[:, 2:3], in1=b.st.ap[:, 2:3], op=ALU.mult))
                sch.op("dve", [st], [st],
                       lambda e, b=b: e.scalar_tensor_tensor(out=b.st.ap[:, 4:5], in0=b.st.ap[:, 1:2], scalar=1.0 / RET_DV, in1=b.st.ap[:, 3:4],
                                                             op0=ALU.mult, op1=ALU.subtract))
                sch.op("act", [st, eps_t], [st],
                       lambda e, b=b: e.activation(out=b.st.ap[:, 5:6], in_=b.st.ap[:, 4:5], func=AF.Ln, bias=eps_t.ap[:, 0:1]))
                sch.op("act", [st], [st],
                       lambda e, b=b: e.activation(out=b.st.ap[:, 6:7], in_=b.st.ap[:, 5:6], func=AF.Exp, scale=-0.5))
                sch.op("dve", [po, st], [b.xn],
                       lambda e, po=po, b=b: e.tensor_scalar(out=b.xn.ap[:], in0=po.ap[:, :], scalar1=b.st.ap[:, 2:3], scalar2=b.st.ap[:, 6:7],
                                                             op0=ALU.add, op1=ALU.mult))
                sch.op("pool", [b.xn, gg], [b.xn], lambda e, b=b, gg=gg: e.tensor_tensor(out=b.xn.ap[:], in0=b.xn.ap[:], in1=gg.ap[:], op=ALU.mult))
                sch.op("pool", [b.xn, gb], [b.xn], lambda e, b=b, gb=gb: e.tensor_tensor(out=b.xn.ap[:], in0=b.xn.ap[:], in1=gb.ap[:], op=ALU.add))
                sch.op("dve", [b.xn, b.sg], [b.og], lambda e, b=b: e.tensor_tensor(out=b.og.ap[:], in0=b.xn.ap[:], in1=b.sg.ap[:], op=ALU.mult))
                pu = pbig()

                def umm(e, pu=pu, b=b):
                    e.matmul(pu.ap[:, 0:512], lhsT=b.kd2.ap[:, 0:128], rhs=b.v.ap[:], start=True, stop=True)
                    return e.matmul(pu.ap[:, 512:1024], lhsT=b.kd2.ap[:, 128:256], rhs=b.v.ap[:], start=True, stop=True)

                sch.op("pe", [b.kd2, b.v], [pu], umm)
                sch.op("dve", [pu, retdec, S32], [S32],
                       lambda e, pu=pu, h=h: e.scalar_tensor_tensor(out=S32.ap[:], in0=S32.ap[:],
                                                                    scalar=retdec.ap[:, h:h + 1], in1=pu.ap[:, :], op0=ALU.mult, op1=ALU.add))
                sch.op("pool", [S32], [Snxt], lambda e, Snxt=Snxt: e.tensor_copy(out=Snxt.ap[:], in_=S32.ap[:]))
                pg2 = psmall()
                pg2b = pg2.ap[:].bitcast(BF16)

                def otr(e, pg2b=pg2b, b=b):
                    ins = None
                    for i in range(RET_DV // 128):
                        ins = e.transpose(out=pg2b[:, i * 128:(i + 1) * 128], in_=b.og.ap[:, i * 128:(i + 1) * 128], identity=ident.ap[:])
                    return ins

                sch.op("pe", [b.og, ident], [pg2], otr)
                sch.op("act", [pg2], [b.ogT], lambda e, pg2b=pg2b, b=b: e.activation(out=b.ogT.ap[:], in_=pg2b[:, 0:RET_DV], func=AF.Copy))
                store("sp", T_og[1][c], b.ogT, T_og[1][c].ap[:, h * RET_DV:(h + 1) * RET_DV], b.ogT.ap[:])
        sch.barrier()
        release(mL)

    if 0 in layers:
        layer_gla()
        if _DBG_STAGE > 3:
            phase_C(0, "gla_w_out", T_x, "ret_norm", final=False)
    if 1 in layers:
        layer_ret()
        phase_C(1, "ret_w_out", T_h1, "final_norm", final=True)
    sch.barrier()

    with nc.Block() as block:
        sch.emit(block)
    stack.close()
    return nc


_PROG = {}


def _get_prog(NCH, layers, debug=False):
    key = (NCH, tuple(layers), debug)
    if key not in _PROG:
        _PROG[key] = build_program(NCH, layers, debug)
    return _PROG[key]


def _in_maps(inputs, NCH, ncores):
    consts = _consts()
    S = NCH * 128
    maps = []
    for b in range(ncores):
        m = {"x": np.ascontiguousarray(inputs["x"][b, :S]),
             "positions": np.ascontiguousarray(inputs["positions"][b:b + 1, :S]).astype(np.int32)}
        for k in _INPUT_SPECS:
            m[k] = np.ascontiguousarray(np.asarray(inputs[k], dtype=np.float32).reshape(_INPUT_SPECS[k]))
        m.update(consts)
        maps.append(m)
    return maps


def kernel(**inputs):
    inputs = {k: np.asarray(v) for k, v in inputs.items()}
    B = inputs["x"].shape[0]
    nc = _get_prog(SEQ // 128, (0, 1))
    res = run_bass_kernel_spmd(nc, _in_maps(inputs, SEQ // 128, B), core_ids=list(range(B)))
    out = np.stack([np.asarray(r["y"], dtype=np.float32) for r in res.results], axis=0)
    return out
```
